# Optimizing a Trainium2 kernel written in Bass

```python
import jax, jax.numpy as jnp
from jax import lax
import numpy as np

D_MODEL = 2048
BATCH = 1
SEQ = 8192
DEPTH = 2

GRID_W = 64
CTX_LEN = 256
MIX_WIDTH = D_MODEL
HEAD_DIM = 128
A_Q_HEADS = (MIX_WIDTH // 2) // HEAD_DIM
A_KV_HEADS = 2
A_GROUP = A_Q_HEADS // A_KV_HEADS
A_Q_DIM = A_Q_HEADS * HEAD_DIM
A_KV_DIM = A_KV_HEADS * HEAD_DIM
WINDOW = 128
BLOCK = 128
ROPE_THETA = 10000.0
AXIS_DIM = HEAD_DIM // 2
ATTN_SCALE = HEAD_DIM ** -0.5
NEG_INF = -1e30
B_GROUPS = 4
B_WIDTH = MIX_WIDTH // 2
B_GROUP_DIM = B_WIDTH // B_GROUPS
POOL_WINDOWS = (2, 4, 8, 16)
AB_IN = A_Q_DIM + 2 * A_KV_DIM + B_WIDTH
C_WIDTH = MIX_WIDTH // 2
C_GROUPS = 4
C_GROUP_DIM = C_WIDTH // C_GROUPS
CHUNK = 128
D_WIDTH = MIX_WIDTH - C_WIDTH
D_GROUPS = 8
D_GROUP_DIM = D_WIDTH // D_GROUPS
CD_IN = 2 * C_WIDTH + D_WIDTH
D_FF = ((8 * D_MODEL // 3 + 255) // 256) * 256
N_EVEN = (DEPTH + 1) // 2
N_ODD = DEPTH // 2
EPS = 1e-6

kernel_name = "hybrid_diffusion_window_pool_gmlp_fourier"


def _rms(x, g):
    xf = x.astype(jnp.float32)
    y = xf * lax.rsqrt(jnp.mean(xf * xf, axis=-1, keepdims=True) + EPS)
    return (y * g.astype(jnp.float32)).astype(x.dtype)


def _modulate(h, shift, scale):
    return h * (1 + scale) + shift


def _axial_angles(n):
    rows = n // GRID_W
    row = jnp.repeat(jnp.arange(rows, dtype=jnp.float32), GRID_W)
    col = jnp.tile(jnp.arange(GRID_W, dtype=jnp.float32), rows)
    inv = ROPE_THETA ** (-jnp.arange(0, AXIS_DIM, 2, dtype=jnp.float32) / AXIS_DIM)
    return row[:, None] * inv[None, :], col[:, None] * inv[None, :]


def _rope_half(x, ang):
    x1, x2 = jnp.split(x, 2, axis=-1)
    cos = jnp.cos(ang)[:, None, :]
    sin = jnp.sin(ang)[:, None, :]
    return jnp.concatenate([x1 * cos - x2 * sin, x2 * cos + x1 * sin], axis=-1)


def _rope_2d(x, ang_r, ang_c):
    xf = x.astype(jnp.float32)
    xr, xc = jnp.split(xf, 2, axis=-1)
    return jnp.concatenate([_rope_half(xr, ang_r), _rope_half(xc, ang_c)], axis=-1).astype(x.dtype)


def _window_attention(q, k, v, kc, vc, sink):
    B, N = q.shape[0], q.shape[1]
    nb = N // BLOCK
    L = kc.shape[1]
    qb = q.reshape(B, nb, BLOCK, A_KV_HEADS, A_GROUP, HEAD_DIM)

    def band(t):
        tp = jnp.pad(t, ((0, 0), (BLOCK, BLOCK), (0, 0), (0, 0))).reshape(B, nb + 2, BLOCK, A_KV_HEADS, HEAD_DIM)
        return jnp.concatenate([tp[:, :-2], tp[:, 1:-1], tp[:, 2:]], axis=2)

    kw, vw = band(k), band(v)
    qpos = jnp.arange(N).reshape(nb, BLOCK)
    kpos = (jnp.arange(nb)[:, None] - 1) * BLOCK + jnp.arange(3 * BLOCK)[None, :]
    valid = ((jnp.abs(qpos[:, :, None] - kpos[:, None, :]) <= WINDOW)
             & (kpos[:, None, :] >= 0) & (kpos[:, None, :] < N))
    s_band = jnp.einsum('bnqhgd,bnkhd->bnhgqk', qb, kw).astype(jnp.float32) * ATTN_SCALE
    s_band = jnp.where(valid[None, :, None, None], s_band, NEG_INF)
    s_ctx = jnp.einsum('bnqhgd,blhd->bnhgql', qb, kc).astype(jnp.float32) * ATTN_SCALE
    s_sink = jnp.broadcast_to(sink.astype(jnp.float32).reshape(1, 1, A_KV_HEADS, A_GROUP, 1, 1),
                              (B, nb, A_KV_HEADS, A_GROUP, BLOCK, 1))
    pr = jax.nn.softmax(jnp.concatenate([s_band, s_ctx, s_sink], axis=-1), axis=-1).astype(v.dtype)
    o = (jnp.einsum('bnhgqk,bnkhd->bnqhgd', pr[..., :3 * BLOCK], vw)
         + jnp.einsum('bnhgql,blhd->bnqhgd', pr[..., 3 * BLOCK:3 * BLOCK + L], vc))
    return o.reshape(B, N, A_Q_DIM)


def _ctx_attention(q, k, v, sink):
    B, L = q.shape[0], q.shape[1]
    qg = q.reshape(B, L, A_KV_HEADS, A_GROUP, HEAD_DIM)
    s = jnp.einsum('blhgd,bmhd->bhglm', qg, k).astype(jnp.float32) * ATTN_SCALE
    s_sink = jnp.broadcast_to(sink.astype(jnp.float32).reshape(1, A_KV_HEADS, A_GROUP, 1, 1),
                              (B, A_KV_HEADS, A_GROUP, L, 1))
    pr = jax.nn.softmax(jnp.concatenate([s, s_sink], axis=-1), axis=-1)[..., :L].astype(v.dtype)
    o = jnp.einsum('bhglm,bmhd->blhgd', pr, v)
    return o.reshape(B, L, A_Q_DIM)


def _pool_mix(z, w_pool, pool_scale):
    B, N = z.shape[0], z.shape[1]
    zf = z.astype(jnp.float32)
    cs = jnp.concatenate([jnp.zeros((B, 1, B_WIDTH), jnp.float32), jnp.cumsum(zf, axis=1)], axis=1)
    cs = cs.reshape(B, N + 1, B_GROUPS, B_GROUP_DIM)
    t = jnp.arange(N)
    half = jnp.array(POOL_WINDOWS, dtype=jnp.int32) // 2
    lo = jnp.clip(t[:, None] - half[None, :], 0, N)
    hi = jnp.clip(t[:, None] + half[None, :], 0, N)
    gidx = jnp.arange(B_GROUPS)[None, :]
    mean = (cs[:, hi, gidx] - cs[:, lo, gidx]) / (hi - lo).astype(jnp.float32)[None, :, :, None]
    d = (mean - zf.reshape(B, N, B_GROUPS, B_GROUP_DIM)).astype(z.dtype)
    y = jnp.einsum('bngc,gcd->bngd', d, w_pool).reshape(B, N, B_WIDTH)
    return y * pool_scale


def _even_mix(h, hc, ang_r, ang_c, w_in, qn_g, kn_g, sink, w_pool, pool_scale, w_out, need_ctx):
    B, N = h.shape[0], h.shape[1]
    L = hc.shape[1]
    p = h @ w_in
    q = _rms(p[..., :A_Q_DIM].reshape(B, N, A_Q_HEADS, HEAD_DIM), qn_g)
    k = _rms(p[..., A_Q_DIM:A_Q_DIM + A_KV_DIM].reshape(B, N, A_KV_HEADS, HEAD_DIM), kn_g)
    v = p[..., A_Q_DIM + A_KV_DIM:A_Q_DIM + 2 * A_KV_DIM].reshape(B, N, A_KV_HEADS, HEAD_DIM)
    q = _rope_2d(q, ang_r, ang_c)
    k = _rope_2d(k, ang_r, ang_c)
    pkv = hc @ w_in[:, A_Q_DIM:A_Q_DIM + 2 * A_KV_DIM]
    kc = _rms(pkv[..., :A_KV_DIM].reshape(B, L, A_KV_HEADS, HEAD_DIM), kn_g)
    vc = pkv[..., A_KV_DIM:].reshape(B, L, A_KV_HEADS, HEAD_DIM)
    y = jnp.concatenate([_window_attention(q, k, v, kc, vc, sink),
                         _pool_mix(p[..., A_Q_DIM + 2 * A_KV_DIM:], w_pool, pool_scale)], axis=-1) @ w_out
    if not need_ctx:
        return y, None
    qc = _rms((hc @ w_in[:, :A_Q_DIM]).reshape(B, L, A_Q_HEADS, HEAD_DIM), qn_g)
    yc = jnp.concatenate([_ctx_attention(qc, kc, vc, sink),
                          _pool_mix(hc @ w_in[:, A_Q_DIM + 2 * A_KV_DIM:], w_pool, pool_scale)], axis=-1) @ w_out
    return y, yc


def _fourier_mix(f, w_fourier):
    B, N = f.shape[0], f.shape[1]
    fg = f.astype(jnp.float32).reshape(B, N, D_GROUPS, D_GROUP_DIM)
    z = jnp.fft.fftn(fg, axes=(1, 3), norm='ortho').real.astype(f.dtype)
    return z.reshape(B, N, D_WIDTH) @ w_fourier


def _odd_mix(h, w_in, v_norm_g, w_spatial, b_spatial, w_fourier, w_out):
    B, N = h.shape[0], h.shape[1]
    nc = N // CHUNK
    p = h @ w_in
    u = jax.nn.gelu(p[..., :C_WIDTH], approximate=False)
    v = _rms(jax.nn.gelu(p[..., C_WIDTH:2 * C_WIDTH], approximate=False), v_norm_g)
    vc = v.reshape(B, nc, CHUNK, C_GROUPS, C_GROUP_DIM)
    s = jnp.einsum('gpq,bkqgc->bkpgc', w_spatial, vc) + b_spatial.T[:, :, None]
    c_out = u * s.reshape(B, N, C_WIDTH)
    d_out = _fourier_mix(p[..., 2 * C_WIDTH:], w_fourier)
    return jnp.concatenate([c_out, d_out], axis=-1) @ w_out


def _dwconv3(u, w, b):
    up = jnp.pad(u, ((0, 0), (1, 1), (0, 0)))
    return up[:, :-2] * w[0] + up[:, 1:-1] * w[1] + up[:, 2:] * w[2] + b


def _conv_ffn(h, w_up, conv_w, conv_b, w_down):
    u = _dwconv3(h @ w_up, conv_w, conv_b)
    g, val = jnp.split(u, 2, axis=-1)
    return (jax.nn.silu(g) * val) @ w_down


def setup_inputs(seed: int = 0) -> dict:
    key = jax.random.key(seed)
    ks = iter(jax.random.split(key, 32))

    def nrm(shape, scale):
        return jax.random.normal(next(ks), shape, jnp.float32) * scale

    def gain(shape):
        return 1.0 + nrm(shape, 0.02)

    return {
        'x': nrm((BATCH, SEQ, D_MODEL), 1.0),
        'c': nrm((BATCH, D_MODEL), 1.0),
        'ctx': nrm((BATCH, CTX_LEN, D_MODEL), 1.0),
        'c_ctx': nrm((D_MODEL,), 1.0),
        'w_mod': nrm((DEPTH, D_MODEL, 6 * D_MODEL), D_MODEL ** -0.5),
        'b_mod': nrm((DEPTH, 6 * D_MODEL), 0.02),
        'norm1_g': gain((DEPTH, D_MODEL)),
        'norm2_g': gain((DEPTH, D_MODEL)),
        'ab_w_in': nrm((N_EVEN, D_MODEL, AB_IN), D_MODEL ** -0.5),
        'a_q_norm_g': gain((N_EVEN, HEAD_DIM)),
        'a_k_norm_g': gain((N_EVEN, HEAD_DIM)),
        'a_sink': nrm((N_EVEN, A_Q_HEADS), 0.5),
        'b_w_pool': nrm((N_EVEN, B_GROUPS, B_GROUP_DIM, B_GROUP_DIM), B_GROUP_DIM ** -0.5),
        'b_pool_scale': gain((N_EVEN, B_WIDTH)),
        'ab_w_out': nrm((N_EVEN, MIX_WIDTH, D_MODEL), MIX_WIDTH ** -0.5),
        'cd_w_in': nrm((N_ODD, D_MODEL, CD_IN), D_MODEL ** -0.5),
        'c_v_norm_g': gain((N_ODD, C_WIDTH)),
        'c_w_spatial': nrm((N_ODD, C_GROUPS, CHUNK, CHUNK), CHUNK ** -0.5),
        'c_b_spatial': gain((N_ODD, C_GROUPS, CHUNK)),
        'd_w_fourier': nrm((N_ODD, D_WIDTH, D_WIDTH), D_WIDTH ** -0.5),
        'cd_w_out': nrm((N_ODD, MIX_WIDTH, D_MODEL), MIX_WIDTH ** -0.5),
        'f_w_up': nrm((DEPTH, D_MODEL, 2 * D_FF), D_MODEL ** -0.5),
        'f_conv_w': nrm((DEPTH, 3, 2 * D_FF), 3 ** -0.5),
        'f_conv_b': nrm((DEPTH, 2 * D_FF), 0.02),
        'f_w_down': nrm((DEPTH, D_FF, D_MODEL), D_FF ** -0.5),
    }


def reference(x, c, ctx, c_ctx, w_mod, b_mod, norm1_g, norm2_g, ab_w_in, a_q_norm_g, a_k_norm_g, a_sink,
              b_w_pool, b_pool_scale, ab_w_out, cd_w_in, c_v_norm_g, c_w_spatial, c_b_spatial, d_w_fourier,
              cd_w_out, f_w_up, f_conv_w, f_conv_b, f_w_down):
    ang_r, ang_c = _axial_angles(x.shape[1])
    for layer in range(DEPTH):
        need_ctx = layer < DEPTH - 1
        is_even = layer % 2 == 0
        i = layer // 2
        mod = jax.nn.silu(c) @ w_mod[layer] + b_mod[layer]
        ml = jnp.split(mod[:, None, :], 6, axis=-1)
        mod_c = jax.nn.silu(c_ctx) @ w_mod[layer] + b_mod[layer]
        mc = jnp.split(mod_c[None, None, :], 6, axis=-1)
        h = _modulate(_rms(x, norm1_g[layer]), ml[0], ml[1])
        if is_even:
            hc = _modulate(_rms(ctx, norm1_g[layer]), mc[0], mc[1])
            y, yc = _even_mix(h, hc, ang_r, ang_c, ab_w_in[i], a_q_norm_g[i], a_k_norm_g[i], a_sink[i],
                              b_w_pool[i], b_pool_scale[i], ab_w_out[i], need_ctx)
        else:
            y = _odd_mix(h, cd_w_in[i], c_v_norm_g[i], c_w_spatial[i], c_b_spatial[i], d_w_fourier[i], cd_w_out[i])
            if need_ctx:
                hc = _modulate(_rms(ctx, norm1_g[layer]), mc[0], mc[1])
                yc = _odd_mix(hc, cd_w_in[i], c_v_norm_g[i], c_w_spatial[i], c_b_spatial[i], d_w_fourier[i], cd_w_out[i])
        x = x + ml[2] * y
        x = x + ml[5] * _conv_ffn(_modulate(_rms(x, norm2_g[layer]), ml[3], ml[4]),
                                  f_w_up[layer], f_conv_w[layer], f_conv_b[layer], f_w_down[layer])
        if need_ctx:
            ctx = ctx + mc[2] * yc
            ctx = ctx + mc[5] * _conv_ffn(_modulate(_rms(ctx, norm2_g[layer]), mc[3], mc[4]),
                                          f_w_up[layer], f_conv_w[layer], f_conv_b[layer], f_w_down[layer])
    return x
```

```python
import contextlib
import numpy as np
import concourse.bass as bass
import concourse.mybir as mybir
from concourse.bass_utils import run_bass_kernel_spmd

F32 = mybir.dt.float32
BF16 = mybir.dt.bfloat16
AF = mybir.ActivationFunctionType
ALU = mybir.AluOpType
AX = mybir.AxisListType

NCORES = 8
D = 2048
SEQ = 8192
T = SEQ // NCORES
KT = D // 128
HALO = 128
TH = T + 2 * HALO
DFF = 5632
NPAIR = DFF // 128
EPS = 1e-6
NEG = -30000.0

ENGS = ("pe", "act", "dve", "pool", "sp")


class Op:
    __slots__ = ("eng", "fn", "deps", "is_dma", "key", "ticket", "has_dep", "name")

    def __init__(self, eng, fn, deps, is_dma, key, name):
        self.eng = eng
        self.fn = fn
        self.deps = deps
        self.is_dma = is_dma
        self.key = key
        self.ticket = None
        self.has_dep = False
        self.name = name


class Sched:
    def __init__(self, nc):
        self.nc = nc
        self.ops = {e: [] for e in ENGS}
        self.last_w = {}
        self.readers = {}
        self.retired = []
        self.pending = {}
        self.final_waits = []

    def _key_name(self, k):
        return k[0] if isinstance(k, tuple) else k

    def _track(self, o, reads, writes):
        deps = list(o.deps)
        for k in list(reads) + list(writes):
            if k not in self.last_w and k not in self.readers:
                deps.extend(self.pending.get(self._key_name(k), ()))
        for k in reads:
            w = self.last_w.get(k)
            if w is not None:
                deps.append(w)
        for k in writes:
            w = self.last_w.get(k)
            if w is not None:
                deps.append(w)
            deps.extend(self.readers.get(k, ()))
        for k in reads:
            self.readers.setdefault(k, []).append(o)
        for k in writes:
            self.last_w[k] = o
            self.readers[k] = []
        seen = set()
        out = []
        for d in deps:
            if d is o or id(d) in seen:
                continue
            seen.add(id(d))
            out.append(d)
        o.deps = out
        for d in out:
            d.has_dep = True

    def op(self, eng, fn, reads=(), writes=(), deps=(), name=""):
        o = Op(eng, fn, list(deps), False, None, name)
        self._track(o, reads, writes)
        self.ops[eng].append(o)
        return o

    def dma(self, eng, fn, key, reads=(), writes=(), deps=(), name=""):
        o = Op(eng, fn, list(deps), True, key, name)
        self._track(o, reads, writes)
        self.ops[eng].append(o)
        return o

    def retire(self, name):
        for k in list(self.last_w.keys()):
            if self._key_name(k) == name:
                self.retired.append(self.last_w.pop(k))
        for k in list(self.readers.keys()):
            if self._key_name(k) == name:
                self.retired.extend(self.readers.pop(k))

    def fresh(self, name):
        seen = set()
        lst = []
        for o in self.retired:
            if id(o) not in seen:
                seen.add(id(o))
                lst.append(o)
        self.retired = lst
        self.pending[name] = list(lst)

    def wait_all_at_end(self, eng, ops):
        for o in ops:
            o.has_dep = True
        self.final_waits.append((eng, list(ops)))

    def emit(self):
        nc = self.nc
        with contextlib.ExitStack() as st:
            sems = {}
            for e in ENGS:
                sems[e] = st.enter_context(nc.semaphore("s_" + e))
            for e in ENGS:
                for o in self.ops[e]:
                    if o.is_dma and o.key not in sems:
                        sems[o.key] = st.enter_context(nc.semaphore("d_%d" % len(sems)))
            cnt = {k: 0 for k in sems}
            for e in ENGS:
                for o in self.ops[e]:
                    if o.is_dma:
                        cnt[o.key] += 16
                        o.ticket = (o.key, cnt[o.key])
                    elif o.has_dep:
                        cnt[e] += 1
                        o.ticket = (e, cnt[e])
            block = st.enter_context(nc.Block())
            sched = self

            def run(e, eng):
                waited = {}

                def do_waits(deps):
                    need = {}
                    for d in deps:
                        if d.eng == "pe" and e == "pe" and not d.is_dma:
                            continue
                        k, v = d.ticket
                        if v > need.get(k, 0):
                            need[k] = v
                    for k, v in need.items():
                        if v > waited.get(k, 0):
                            eng.wait_ge(sems[k], v)
                            waited[k] = v

                for o in sched.ops[e]:
                    do_waits(o.deps)
                    ins = o.fn(eng)
                    if o.is_dma:
                        ins.then_inc(sems[o.key], 16)
                    elif o.has_dep:
                        ins.then_inc(sems[e], 1)
                for (fe, fops) in sched.final_waits:
                    if fe == e:
                        do_waits(fops)

            @block.tensor
            def _(eng):
                run("pe", eng)

            @block.scalar
            def _(eng):
                run("act", eng)

            @block.vector
            def _(eng):
                run("dve", eng)

            @block.gpsimd
            def _(eng):
                run("pool", eng)

            @block.sync
            def _(eng):
                run("sp", eng)


WSLOT = 4096
NW = 4
ARENA = 103936


class Builder:
    def __init__(self):
        self.nc = bass.Bass("TRN2", target_bir_lowering=False)
        self.s = Sched(self.nc)
        self.root = contextlib.ExitStack()
        self.outs = []
        nc = self.nc
        self.ps = [self.root.enter_context(nc.psum_tensor("ps%d" % i, [128, 512], F32)) for i in range(8)]
        self.ps_i = 0
        self.arena = self.root.enter_context(nc.sbuf_tensor("arena", [128, ARENA], BF16))
        self.free_list = [(0, ARENA)]
        self.allocs = {}
        self.w = [self.sb("w%d" % i, [WSLOT], BF16) for i in range(NW)]
        self.w_i = 0
        self.peak = 0

    def din(self, name, shape, dt=F32):
        return self.nc.dram_tensor(name, list(shape), dt, kind="ExternalInput").ap()

    def dout(self, name, shape, dt=F32):
        return self.nc.dram_tensor(name, list(shape), dt, kind="ExternalOutput").ap()

    def dtmp(self, name, shape, dt=F32):
        return self.nc.dram_tensor(name, list(shape), dt, kind="Internal").ap()

    def sb(self, name, fshape, dt, parts=128):
        n = int(np.prod(fshape))
        units = n * (2 if dt == F32 else 1)
        units = (units + 31) // 32 * 32
        for idx, (off, sz) in enumerate(self.free_list):
            if sz >= units:
                break
        else:
            raise RuntimeError("arena OOM for %s (%d units); free=%s" % (name, units, self.free_list))
        if sz == units:
            self.free_list.pop(idx)
        else:
            self.free_list[idx] = (off + units, sz - units)
        assert name not in self.allocs, name
        self.allocs[name] = (off, units)
        v = self.arena[0:parts, off:off + n * (2 if dt == F32 else 1)]
        if dt == F32:
            v = v.bitcast(F32)
        if len(fshape) == 2:
            v = v.rearrange("p (a b) -> p a b", b=fshape[1])
        elif len(fshape) == 3:
            v = v.rearrange("p (a b c) -> p a b c", b=fshape[1], c=fshape[2])
        self.s.fresh(name)
        used = ARENA - sum(sz for _, sz in self.free_list)
        self.peak = max(getattr(self, "peak", 0), used)
        return v

    def free(self, *names):
        for name in names:
            off, units = self.allocs.pop(name)
            self.s.retire(name)
            self.free_list.append((off, units))
        self.free_list.sort()
        merged = []
        for off, sz in self.free_list:
            if merged and merged[-1][0] + merged[-1][1] == off:
                merged[-1] = (merged[-1][0], merged[-1][1] + sz)
            else:
                merged.append((off, sz))
        self.free_list = merged

    def psum(self):
        i = self.ps_i
        self.ps_i = (i + 1) % 8
        return self.ps[i], ("ps", i)

    def wload(self, src, n):
        i = self.w_i
        self.w_i = (i + 1) % NW
        wt = self.w[i]
        self.s.dma("pool", lambda e: e.dma_start(out=wt[:, 0:n], in_=src), key=("w", i), writes=[("w%d" % i)])
        return wt, "w%d" % i

    def finish(self):
        self.s.wait_all_at_end("sp", self.outs)
        self.s.emit()
        self.root.close()
        return self.nc

    def store(self, dst, src, reads, key="out"):
        o = self.s.dma("sp", lambda e: e.dma_start(out=dst, in_=src), key=key, reads=reads)
        self.outs.append(o)
        return o

    def load(self, dst, src, writes, key, eng="sp"):
        return self.s.dma(eng, lambda e: e.dma_start(out=dst, in_=src), key=key, writes=writes)


def pvec(v):
    v = np.asarray(v, np.float32)
    return np.ascontiguousarray(v.reshape(-1, 128).T)


def tile_w(w, blocks):
    out = []
    for kts, cols in blocks:
        sub = w[:, cols]
        sub = sub.reshape(-1, 128, len(cols))[kts]
        out.append(np.ascontiguousarray(sub.transpose(1, 0, 2)).reshape(128, -1))
    return np.ascontiguousarray(np.stack(out, 0).astype(np.float32))


def build_mod(b):
    nc, s = b.nc, b.s
    wm = b.din("wm", [24, 128, 2048])
    cvec = b.din("cvec", [128, 16, 2])
    bm = b.din("bm", [128, 24])
    modo = b.dout("modo", [128, 24, 2])
    cv = b.sb("cv", [16, 2], F32)
    cs = b.sb("cs", [16, 2], BF16)
    bmt = b.sb("bmt", [24], F32)
    mo = b.sb("mo", [24, 2], F32)
    b.load(cv, cvec, ["cv"], "ld0")
    b.load(bmt, bm, ["bmt"], "ld1")
    s.op("act", lambda e: e.activation(out=cs, in_=cv, func=AF.Silu), reads=["cv"], writes=["cs"])
    pt, pk = b.psum()
    for blk in range(24):
        wt, wk = b.wload(wm[blk], 2048)
        for kt in range(16):
            s.op("pe", lambda e, blk=blk, kt=kt, wt=wt: e.matmul(
                pt[:, blk * 2:blk * 2 + 2], wt[:, kt * 128:(kt + 1) * 128], cs[:, kt, :],
                start=(kt == 0), stop=(kt == 15)), reads=[wk, "cs"], writes=[pk])
    pv = pt[:, 0:48].rearrange("p (j v) -> p j v", v=2)
    for v in range(2):
        s.op("dve", lambda e, v=v: e.tensor_tensor(out=mo[:, :, v], in0=pv[:, :, v], in1=bmt, op=ALU.add),
             reads=[pk, "bmt"], writes=["mo"])
    b.store(modo, mo, ["mo"])


def host_mod_inputs(inp):
    w_mod, b_mod = inp["w_mod"], inp["b_mod"]
    c, c_ctx = inp["c"], inp["c_ctx"]
    cvec = np.stack([pvec(c[0]), pvec(c_ctx)], axis=-1)
    maps = []
    for r in range(NCORES):
        blocks = []
        bm = np.zeros((128, 24), np.float32)
        wm = np.empty((24, 128, 2048), np.float32)
        for l in range(2):
            for jj in range(12):
                j = 12 * r + jj
                sub = w_mod[l][:, j * 128:(j + 1) * 128]
                wm[l * 12 + jj] = sub.reshape(16, 128, 128).transpose(1, 0, 2).reshape(128, 2048)
                bm[:, l * 12 + jj] = b_mod[l, j * 128:(j + 1) * 128]
        maps.append({"wm": wm, "cvec": np.ascontiguousarray(cvec), "bm": bm})
    return maps


def gather_mod(results):
    modv = np.empty((128, 2, 96, 2), np.float32)
    for r in range(NCORES):
        mo = results[r]["modo"].reshape(128, 2, 12, 2)
        modv[:, :, 12 * r:12 * r + 12, :] = mo
    return modv


def tkeys(name, lo, hi, g=128):
    return [(name, i) for i in range(lo // g, (hi + g - 1) // g)]


class Ring:
    def __init__(self, b, name, n, fshape, dt):
        self.views = [b.sb("%s%d" % (name, i), fshape, dt) for i in range(n)]
        self.names = ["%s%d" % (name, i) for i in range(n)]
        self.i = 0
        self.b = b

    def next(self):
        i = self.i
        self.i = (i + 1) % len(self.views)
        return self.views[i], self.names[i]

    def free(self):
        self.b.free(*self.names)


V_G1 = (0, 32)
V_G2 = (16, 48)
V_GQ, V_GK = 64, 65
V_PS = 66
V_SINK = 74
V_FL, V_FR = 82, 83
V_OHL, V_OHR = 84, 92
NV = 100

C_ONES, C_ID, C_RT = 0, 128, 256
C_MPF, C_MP, C_MN, C_MNL = 384, 896, 1408, 1920
C_PB = 2432
NCB = C_PB + 36 * 128


def load_common(b, need_cbf=True):
    c = {}
    s = b.s
    d_modv = b.din("modv", [128, 2, 96, 2])
    d_vecs = b.din("vecs", [128, NV])
    c["modv"] = b.sb("modv", [2, 96, 2], F32)
    c["vecs"] = b.sb("vecs", [NV], F32)
    b.load(c["modv"], d_modv, ["modv"], "ldc0")
    b.load(c["vecs"], d_vecs, ["vecs"], "ldc1")
    c["epsc"] = b.sb("epsc", [4], F32)
    for i, val in enumerate((D * EPS, 128 * EPS, 1024 * EPS, 0.0)):
        s.op("dve", lambda e, i=i, val=val: e.memset(c["epsc"][:, i:i + 1], float(val)), writes=["epsc"])
    d_cb1 = b.din("cb1", [128, 256])
    c["cb1"] = b.sb("cb1", [256], BF16)
    s.dma("pool", lambda e: e.dma_start(out=c["cb1"], in_=d_cb1), key="ldc3", writes=["cb1"])
    c["ones"] = c["cb1"][:, 0:128]
    c["ident"] = c["cb1"][:, 128:256]
    if need_cbf:
        d_cbf = b.din("cbf", [128, NCB])
        c["cbf"] = b.sb("cbf", [NCB], BF16)
        s.dma("pool", lambda e: e.dma_start(out=c["cbf"], in_=d_cbf), key="ldc2", writes=["cbf"])
    return c


def mk_AS(b, c, l, v, i_shift, i_scale, gcol, name, dn=D):
    s = b.s
    A = b.sb(name + "A", [16], F32)
    S = b.sb(name + "S", [16], F32)
    modv, vecs = c["modv"], c["vecs"]
    s.op("dve", lambda e: e.tensor_scalar(out=A, in0=modv[:, l, 16 * i_scale:16 * i_scale + 16, v], scalar1=1.0,
                                          scalar2=float(np.sqrt(dn)), op0=ALU.add, op1=ALU.mult),
         reads=["modv"], writes=[name + "A"])
    s.op("dve", lambda e: e.tensor_tensor(out=A, in0=A, in1=vecs[:, gcol:gcol + 16], op=ALU.mult),
         reads=[name + "A", "vecs"], writes=[name + "A"])
    s.op("dve", lambda e: e.tensor_copy(out=S, in_=modv[:, l, 16 * i_shift:16 * i_shift + 16, v]),
         reads=["modv"], writes=[name + "S"])
    return A, S


def mk_gate(b, c, l, i, name):
    G = b.sb(name, [16], F32)
    modv = c["modv"]
    b.s.op("dve", lambda e: e.tensor_copy(out=G, in_=modv[:, l, 16 * i:16 * i + 16, 0]), reads=["modv"], writes=[name])
    return G


def rstd_evac(b, c, r, rk, pt, pk, n, which):
    s = b.s
    epsc = c["epsc"]
    s.op("act", lambda e: e.activation(out=r[:, 0:n], in_=pt[:, 0:n], func=AF.Sqrt, bias=epsc[:, which:which + 1], scale=1.0),
         reads=[pk, "epsc"], writes=[rk])
    s.op("dve", lambda e: e.reciprocal(out=r[:, 0:n], in_=r[:, 0:n]), reads=[rk], writes=[rk])


def norm_mod(b, c, chunks, A, S, An, Sn, rings, dn=D):
    s = b.s
    ones = c["ones"]
    sqr, tmpr, rr = rings
    for (src, skeys, dst, dkeys, n) in chunks:
        pt, pk = b.psum()
        for t in range(16):
            sq, sqk = sqr.next()
            s.op("act", lambda e, sq=sq, t=t, src=src, n=n: e.activation(out=sq[:, 0:n], in_=src[:, t, :], func=AF.Square),
                 reads=skeys, writes=[sqk])
            s.op("pe", lambda e, sq=sq, t=t, pt=pt, n=n: e.matmul(pt[:, 0:n], ones, sq[:, 0:n], start=(t == 0), stop=(t == 15)),
                 reads=[sqk, "cb1"], writes=[pk])
        r, rk = rr.next()
        rstd_evac(b, c, r, rk, pt, pk, n, 0 if dn == D else 1)
        for t in range(16):
            tmp, tk = tmpr.next()
            s.op("dve", lambda e, tmp=tmp, t=t, src=src, r=r, n=n: e.scalar_tensor_tensor(
                out=tmp[:, 0:n], in0=src[:, t, :], scalar=A[:, t:t + 1], in1=r[:, 0:n], op0=ALU.mult, op1=ALU.mult),
                 reads=skeys + [rk, An], writes=[tk])
            s.op("act", lambda e, tmp=tmp, t=t, dst=dst, n=n: e.activation(
                out=dst[:, t, :], in_=tmp[:, 0:n], func=AF.Identity, bias=S[:, t:t + 1], scale=1.0),
                 reads=[tk, Sn], writes=dkeys)


def rope_norm_evac(b, c, pt, pk, n, gap, gkey, cos, sin, out, okeys, rings, out3d=False):
    s = b.s
    cbf = c["cbf"]
    ones = c["ones"]
    RT = cbf[:, C_RT:C_RT + 128]
    sqr, tmpr, rr, qgr = rings
    qg, qk = qgr.next()
    sq, sk = sqr.next()
    s.op("act", lambda e: e.activation(out=qg[:, 0:n], in_=pt[:, 0:n], func=AF.Identity, scale=gap), reads=[pk, gkey], writes=[qk])
    s.op("act", lambda e: e.activation(out=sq[:, 0:n], in_=pt[:, 0:n], func=AF.Square), reads=[pk], writes=[sk])
    p2, p2k = b.psum()
    s.op("pe", lambda e: e.matmul(p2[:, 0:n], ones, sq[:, 0:n], start=True, stop=True), reads=[sk, "cb1"], writes=[p2k])
    r, rk = rr.next()
    rstd_evac(b, c, r, rk, p2, p2k, n, 1)
    if cos is None:
        s.op("dve", lambda e: e.tensor_tensor(out=out, in0=qg[:, 0:n], in1=r[:, 0:n], op=ALU.mult),
             reads=[qk, rk], writes=okeys)
        return
    p3, p3k = b.psum()
    s.op("pe", lambda e: e.matmul(p3[:, 0:n], RT, qg[:, 0:n], start=True, stop=True), reads=[qk, "cbf"], writes=[p3k])
    t1, t1k = tmpr.next()
    t2, t2k = tmpr.next()
    s.op("dve", lambda e: e.tensor_tensor(out=t1[:, 0:n], in0=qg[:, 0:n], in1=cos, op=ALU.mult), reads=[qk, "tabs"], writes=[t1k])
    s.op("dve", lambda e: e.tensor_tensor(out=t2[:, 0:n], in0=p3[:, 0:n], in1=sin, op=ALU.mult), reads=[p3k, "tabs"], writes=[t2k])
    s.op("dve", lambda e: e.tensor_tensor(out=t1[:, 0:n], in0=t1[:, 0:n], in1=t2[:, 0:n], op=ALU.add), reads=[t1k, t2k], writes=[t1k])
    if out3d:
        i0 = t1[:, 0:n].rearrange("p (a b) -> p a b", b=128)
        i1 = r[:, 0:n].rearrange("p (a b) -> p a b", b=128)
    else:
        i0, i1 = t1[:, 0:n], r[:, 0:n]
    s.op("dve", lambda e: e.tensor_tensor(out=out, in0=i0, in1=i1, op=ALU.mult), reads=[t1k, rk], writes=okeys)


def phase1(b, c, io):
    s = b.s
    cbf, vecs = c["cbf"], c["vecs"]
    ones = c["ones"]
    ident = c["ident"]
    A1, S1 = mk_AS(b, c, 0, 0, 0, 1, V_G1[0], "n1")
    A1c, S1c = mk_AS(b, c, 0, 1, 0, 1, V_G1[0], "n1c")
    gate1 = mk_gate(b, c, 0, 2, "gate1")
    gks = b.sb("gks", [1], F32)
    s.op("dve", lambda e: e.tensor_scalar(out=gks, in0=vecs[:, V_GK:V_GK + 1], scalar1=float(np.sqrt(128.0)), scalar2=None,
                                          op0=ALU.mult), reads=["vecs"], writes=["gks"])
    sqr = Ring(b, "sq", 2, [512], BF16)
    tmpr = Ring(b, "tmp", 4, [512], F32)
    rr = Ring(b, "rr", 2, [512], F32)
    nrings = (sqr, tmpr, rr)
    h = b.sb("h", [16, TH], BF16)
    hc = b.sb("hc", [16, 256], BF16)
    xs = b.sb("xs", [16, 256], F32)
    xTv = io["xT"].rearrange("(t p) n -> p t n", p=128)
    ctxTv = io["ctxT"].rearrange("(t p) n -> p t n", p=128)
    b.load(xs, ctxTv, ["xs"], "ldx")
    norm_mod(b, c, [(xs, ["xs"], hc, ["hc"], 256)], A1c, S1c, "n1cA", "n1cS", nrings)
    for ci in range(5):
        b.load(xs, xTv[:, :, ci * 256:(ci + 1) * 256], ["xs"], "ldx")
        norm_mod(b, c, [(xs, ["xs"], h[:, :, ci * 256:(ci + 1) * 256], tkeys("h", ci * 256, ci * 256 + 256), 256)],
                 A1, S1, "n1A", "n1S", nrings)
    b.free("xs")

    evi = [0]

    def evac_copy(out, pin, reads, writes):
        evi[0] += 1
        if evi[0] % 2:
            s.op("act", lambda e: e.activation(out=out, in_=pin, func=AF.Copy), reads=reads, writes=writes)
        else:
            s.op("dve", lambda e: e.tensor_copy(out=out, in_=pin), reads=reads, writes=writes)

    Z = b.sb("Z", [10, 1024], BF16)
    for blk in range(4):
        wt, wk = b.wload(io["w_in0"][6 + blk], 4096)
        wv = wt.rearrange("p (k m) -> p k m", m=256)
        for i in range(10):
            pt, pk = b.psum()
            for kt in range(16):
                s.op("pe", lambda e, pt=pt, kt=kt, i=i, wv=wv: e.matmul(pt[:, 0:256], h[:, kt, i * 128:(i + 1) * 128], wv[:, kt, :],
                                                                       start=(kt == 0), stop=(kt == 15)),
                     reads=[("h", i), wk], writes=[pk])
            evac_copy(Z[:, i, blk * 256:(blk + 1) * 256], pt[:, 0:256], [pk], [("Z", i)])
    Dp = b.sb("Dp", [8, 1024], BF16)
    PB = cbf[:, C_PB:C_PB + 36 * 128].rearrange("p (g v d t) -> p g v d t", g=4, v=3, d=3)
    for ct in range(8):
        g = ct // 2
        for half in range(2):
            pt, pk = b.psum()
            for jj in range(4):
                j = half * 4 + jj
                var = 0 if j == 0 else (2 if j == 7 else 1)
                for d in range(3):
                    s.op("pe", lambda e, pt=pt, jj=jj, j=j, d=d, ct=ct, g=g, var=var: e.matmul(
                        pt[:, jj * 128:(jj + 1) * 128], Z[:, j + d, ct * 128:(ct + 1) * 128], PB[:, g, var, d, :],
                        start=(d == 0), stop=(d == 2)), reads=[("Z", j + d), "cbf"], writes=[pk])
            evac_copy(Dp[:, ct, half * 512:(half + 1) * 512], pt[:, :], [pk], [("Dp", ct)])
    b.free("Z")
    CAT = b.sb("CAT", [16, T], BF16)
    for g in range(4):
        wt, wk = b.wload(io["w_pool"][g], 512)
        wv = wt[:, 0:512].rearrange("p (k m) -> p k m", m=256)
        for mm in range(2):
            for half in range(2):
                pt, pk = b.psum()
                for k in range(2):
                    s.op("pe", lambda e, pt=pt, k=k, mm=mm, g=g, half=half, wv=wv: e.matmul(
                        pt[:, :], wv[:, k, mm * 128:(mm + 1) * 128], Dp[:, 2 * g + k, half * 512:(half + 1) * 512],
                        start=(k == 0), stop=(k == 1)), reads=[wk, ("Dp", 2 * g + k)], writes=[pk])
                col = V_PS + 2 * g + mm
                s.op("act", lambda e, pt=pt, g=g, mm=mm, half=half, col=col: e.activation(
                    out=CAT[:, 8 + 2 * g + mm, half * 512:(half + 1) * 512], in_=pt[:, :], func=AF.Identity,
                    scale=vecs[:, col:col + 1]), reads=[pk, "vecs"], writes=[("CAT", 8 + 2 * g + mm)])
    b.free("Dp")

    Krot = b.sb("Krot", [2, TH], BF16)
    KC = b.sb("KC", [2, 256], BF16)
    V = b.sb("V", [10, 256], BF16)
    VC = b.sb("VC", [2, 256], BF16)
    Qrot = b.sb("Qrot", [8, 8, 128], BF16)
    tabs = b.sb("tabs", [2, TH], F32)
    b.load(tabs, io["tabs"], ["tabs"], "ldt")
    qgr = Ring(b, "qg", 2, [512], BF16)
    rrings = (sqr, tmpr, rr, qgr)
    wt, wk = b.wload(io["w_in0"][4], 4096)
    wv = wt.rearrange("p (k m) -> p k m", m=256)
    for g in range(2):
        for (lo, hi) in ((0, 512), (512, 1024), (1024, 1280)):
            n = hi - lo
            pt, pk = b.psum()
            for kt in range(16):
                s.op("pe", lambda e, pt=pt, kt=kt, g=g, lo=lo, hi=hi, n=n, wv=wv: e.matmul(
                    pt[:, 0:n], wv[:, kt, g * 128:(g + 1) * 128], h[:, kt, lo:hi], start=(kt == 0), stop=(kt == 15)),
                     reads=tkeys("h", lo, hi) + [wk], writes=[pk])
            rope_norm_evac(b, c, pt, pk, n, gks[:, 0:1], "gks", tabs[:, 0, lo:hi], tabs[:, 1, lo:hi],
                           Krot[:, g, lo:hi], tkeys("Krot", lo, hi), rrings)
        pt, pk = b.psum()
        for kt in range(16):
            s.op("pe", lambda e, pt=pt, kt=kt, g=g, wv=wv: e.matmul(
                pt[:, 0:256], wv[:, kt, g * 128:(g + 1) * 128], hc[:, kt, :], start=(kt == 0), stop=(kt == 15)),
                 reads=["hc", wk], writes=[pk])
        rope_norm_evac(b, c, pt, pk, 256, gks[:, 0:1], "gks", None, None, KC[:, g, :], ["KC"], rrings)
    wt, wk = b.wload(io["w_in0"][5], 4096)
    wv = wt.rearrange("p (k m) -> p k m", m=256)
    for i in range(10):
        pt, pk = b.psum()
        for kt in range(16):
            s.op("pe", lambda e, pt=pt, kt=kt, i=i, wv=wv: e.matmul(pt[:, 0:256], h[:, kt, i * 128:(i + 1) * 128], wv[:, kt, :],
                                                                   start=(kt == 0), stop=(kt == 15)),
                 reads=[("h", i), wk], writes=[pk])
        evac_copy(V[:, i, :], pt[:, 0:256], [pk], [("V", i)])
    for i in range(2):
        pt, pk = b.psum()
        for kt in range(16):
            s.op("pe", lambda e, pt=pt, kt=kt, i=i, wv=wv: e.matmul(pt[:, 0:256], hc[:, kt, i * 128:(i + 1) * 128], wv[:, kt, :],
                                                                   start=(kt == 0), stop=(kt == 15)),
                 reads=["hc", wk], writes=[pk])
        evac_copy(VC[:, i, :], pt[:, 0:256], [pk], ["VC"])
    for blk in range(4):
        wt, wk = b.wload(io["w_in0"][blk], 4096)
        wv = wt.rearrange("p (k m) -> p k m", m=256)
        for hh in range(2):
            hd = 2 * blk + hh
            for cch in range(2):
                lo = HALO + 512 * cch
                pt, pk = b.psum()
                for kt in range(16):
                    s.op("pe", lambda e, pt=pt, kt=kt, hh=hh, lo=lo, wv=wv: e.matmul(
                        pt[:, :], wv[:, kt, hh * 128:(hh + 1) * 128], h[:, kt, lo:lo + 512], start=(kt == 0), stop=(kt == 15)),
                         reads=tkeys("h", lo, lo + 512) + [wk], writes=[pk])
                rope_norm_evac(b, c, pt, pk, 512, vecs[:, V_GQ:V_GQ + 1], "vecs", tabs[:, 0, lo:lo + 512], tabs[:, 1, lo:lo + 512],
                               Qrot[:, 4 * cch:4 * cch + 4, hd, :], [("Qrot", q) for q in range(4 * cch, 4 * cch + 4)],
                               rrings, out3d=True)
    b.free("h", "hc", "tabs")
    qgr.free()

    Et = Ring(b, "Et", 6, [512], BF16)
    es = b.sb("es", [8], F32)
    esX = b.sb("esX", [2, 4, 128], F32)
    s.op("act", lambda e: e.activation(out=es, in_=vecs[:, V_SINK:V_SINK + 8], func=AF.Exp), reads=["vecs"], writes=["es"])
    for g in range(2):
        s.op("dve", lambda e, g=g: e.tensor_copy(out=esX[:, g], in_=es[:, 4 * g:4 * g + 4].unsqueeze(2).to_broadcast([128, 4, 128])),
             reads=["es"], writes=["esX"])
    for qb in range(8):
        for g in range(2):
            rhsq = Qrot[:, qb, 4 * g:4 * g + 4, :]
            E = []
            for j in range(5):
                pt, pk = b.psum()
                if j < 3:
                    lhsT = Krot[:, g, (qb + j) * 128:(qb + j + 1) * 128]
                    rk = [("Krot", qb + j)]
                else:
                    lhsT = KC[:, g, (j - 3) * 128:(j - 2) * 128]
                    rk = ["KC"]
                masked = j in (0, 2)
                s.op("pe", lambda e, pt=pt, lhsT=lhsT, rhsq=rhsq, masked=masked: e.matmul(pt[:, :], lhsT, rhsq, start=True, stop=(not masked)),
                     reads=rk + [("Qrot", qb)], writes=[pk])
                if masked:
                    if j == 0:
                        mcol = C_MPF if qb == 0 else C_MP
                    else:
                        mcol = C_MNL if qb == 7 else C_MN
                    s.op("pe", lambda e, pt=pt, mcol=mcol: e.matmul(pt[:, :], ident, cbf[:, mcol:mcol + 512], start=False, stop=True),
                         reads=["cbf", "cb1"], writes=[pk])
                et, ek = Et.next()
                s.op("act", lambda e, et=et, pt=pt: e.activation(out=et, in_=pt[:, :], func=AF.Exp), reads=[pk], writes=[ek])
                E.append((et, ek, j))
            po, pok = b.psum()
            pl, plk = b.psum()
            for idx, (et, ek, j) in enumerate(E):
                if j < 3:
                    vt = V[:, qb + j, g * 128:(g + 1) * 128]
                    vk = ("V", qb + j)
                else:
                    vt = VC[:, j - 3, g * 128:(g + 1) * 128]
                    vk = "VC"
                s.op("pe", lambda e, po=po, vt=vt, et=et, idx=idx: e.matmul(po[:, :], vt, et, start=(idx == 0), stop=(idx == 4)),
                     reads=[vk, ek], writes=[pok])
                s.op("pe", lambda e, pl=pl, et=et, idx=idx: e.matmul(pl[:, :], ones, et, start=(idx == 0), stop=(idx == 4)),
                     reads=[ek, "cb1"], writes=[plk])
            den, dk = tmpr.next()
            s.op("dve", lambda e, den=den, pl=pl, g=g: e.tensor_tensor(out=den.rearrange("p (a b) -> p a b", b=128),
                                                                  in0=pl[:, :].rearrange("p (a b) -> p a b", b=128),
                                                                  in1=esX[:, g], op=ALU.add), reads=[plk, "esX"], writes=[dk])
            s.op("dve", lambda e, den=den: e.reciprocal(out=den, in_=den), reads=[dk], writes=[dk])
            s.op("dve", lambda e, den=den, po=po, g=g, qb=qb: e.tensor_tensor(
                out=CAT[:, 4 * g:4 * g + 4, qb * 128:(qb + 1) * 128], in0=po[:, :].rearrange("p (a b) -> p a b", b=128),
                in1=den.rearrange("p (a b) -> p a b", b=128), op=ALU.mult),
                 reads=[pok, dk], writes=[("CAT", 4 * g + hh) for hh in range(4)])
    b.free("Krot", "KC", "V", "VC", "Qrot", "es", "esX")
    Et.free()

    X = b.sb("X", [16, T], F32)
    for t4 in range(4):
        b.load(X[:, 4 * t4:4 * t4 + 4, :], xTv[:, 4 * t4:4 * t4 + 4, HALO:HALO + T], [("X", t) for t in range(4 * t4, 4 * t4 + 4)], "ldX%d" % t4)
    for blk in range(8):
        wt, wk = b.wload(io["w_out0"][blk], 4096)
        wv = wt.rearrange("p (k m) -> p k m", m=256)
        for mm in range(2):
            m = 2 * blk + mm
            for half in range(2):
                pt, pk = b.psum()
                for kt in range(16):
                    s.op("pe", lambda e, pt=pt, kt=kt, mm=mm, half=half, wv=wv: e.matmul(
                        pt[:, :], wv[:, kt, mm * 128:(mm + 1) * 128], CAT[:, kt, half * 512:(half + 1) * 512],
                        start=(kt == 0), stop=(kt == 15)), reads=[wk, ("CAT", kt)], writes=[pk])
                xs_ = X[:, m, half * 512:(half + 1) * 512]
                s.op("dve", lambda e, pt=pt, m=m, xs_=xs_: e.scalar_tensor_tensor(
                    out=xs_, in0=pt[:, :], scalar=gate1[:, m:m + 1], in1=xs_, op0=ALU.mult, op1=ALU.add),
                     reads=[pk, "gate1", ("X", m)], writes=[("X", m)])
    b.free("CAT")
    sqr.free()
    tmpr.free()
    rr.free()
    b.free("n1A", "n1S", "n1cA", "n1cS", "gate1", "gks")
    return X


POOL_W = (2, 4, 8, 16)


def host_tabs(r):
    t = np.arange(TH, dtype=np.int64) + (T * r - HALO)
    t = np.clip(t, 0, SEQ - 1)
    row = (t // 64).astype(np.float32)
    col = (t % 64).astype(np.float32)
    inv = (np.float32(10000.0) ** (-np.arange(0, 64, 2, dtype=np.float32) / np.float32(64))).astype(np.float32)
    ang = np.empty((128, TH), np.float32)
    for p in range(128):
        f = p % 32
        ang[p] = (row if p < 64 else col) * inv[f]
    return np.ascontiguousarray(np.stack([np.cos(ang), np.sin(ang)], axis=1).astype(np.float32))


def host_cbf(r):
    cb = np.zeros((128, NCB), np.float32)
    cb[:, C_ONES:C_ONES + 128] = 1.0
    cb[:, C_ID:C_ID + 128] = np.eye(128, dtype=np.float32)
    RT = np.zeros((128, 128), np.float32)
    for dp in range(128):
        if dp % 64 < 32:
            RT[dp + 32, dp] = -1.0
        else:
            RT[dp - 32, dp] = 1.0
    cb[:, C_RT:C_RT + 128] = RT
    j = np.arange(128)[:, None]
    i = np.arange(128)[None, :]
    mp = np.where(j >= i, 0.0, NEG).astype(np.float32)
    mn = np.where(j <= i, 0.0, NEG).astype(np.float32)
    allneg = np.full((128, 128), NEG, np.float32)
    cb[:, C_MPF:C_MPF + 512] = np.tile(allneg if r == 0 else mp, (1, 4))
    cb[:, C_MP:C_MP + 512] = np.tile(mp, (1, 4))
    cb[:, C_MN:C_MN + 512] = np.tile(mn, (1, 4))
    cb[:, C_MNL:C_MNL + 512] = np.tile(allneg if r == NCORES - 1 else mn, (1, 4))
    PB = np.zeros((128, 4, 3, 3, 128), np.float32)
    bases = (T * r, T * r + 128 if r < NCORES - 1 or True else 0, T * r + 896)
    for g, w in enumerate(POOL_W):
        half = w // 2
        for var, base in enumerate(bases):
            tg = base + np.arange(128)
            lo = np.clip(tg - half, 0, SEQ)
            hi = np.clip(tg + half, 0, SEQ)
            for d in range(3):
                tpg = base + 128 * (d - 1) + np.arange(128)
                inside = (tpg[:, None] >= lo[None, :]) & (tpg[:, None] < hi[None, :])
                blk = inside / (hi - lo)[None, :].astype(np.float32)
                blk = blk - (tpg[:, None] == tg[None, :])
                PB[:, g, var, d, :] = blk
    cb[:, C_PB:] = PB.reshape(128, -1)
    return cb


def host_vecs(inp, r):
    v = np.zeros((128, NV), np.float32)
    for l in range(2):
        v[:, V_G1[l]:V_G1[l] + 16] = pvec(inp["norm1_g"][l])
        v[:, V_G2[l]:V_G2[l] + 16] = pvec(inp["norm2_g"][l])
    v[:, V_GQ] = inp["a_q_norm_g"][0]
    v[:, V_GK] = inp["a_k_norm_g"][0]
    v[:, V_PS:V_PS + 8] = pvec(inp["b_pool_scale"][0])
    v[:, V_SINK:V_SINK + 8] = inp["a_sink"][0][None, :]
    v[:, V_FL] = 0.0 if r == 0 else 1.0
    v[:, V_FR] = 0.0 if r == NCORES - 1 else 1.0
    if r > 0:
        v[:, V_OHL + r - 1] = 1.0
    if r < NCORES - 1:
        v[:, V_OHR + r + 1] = 1.0
    return v


def host_w_l0mix(inp):
    cols = np.arange(256)
    return {
        "w_in0": tile_w(inp["ab_w_in"][0], [(list(range(16)), 256 * bb + cols) for bb in range(10)]),
        "w_pool": np.ascontiguousarray(np.concatenate(
            [tile_w(inp["b_w_pool"][0][g], [(list(range(2)), cols)]) for g in range(4)], axis=0)),
        "w_out0": tile_w(inp["ab_w_out"][0], [(list(range(16)), 256 * bb + cols) for bb in range(8)]),
    }


def host_xT(inp, r):
    x = inp["x"][0]
    xp = np.zeros((TH, D), np.float32)
    lo = T * r - HALO
    hi = lo + TH
    slo, shi = max(lo, 0), min(hi, SEQ)
    xp[slo - lo:shi - lo] = x[slo:shi]
    return np.ascontiguousarray(xp.T)


def build_p1():
    b = Builder()
    c = load_common(b)
    io = {
        "xT": b.din("xT", [D, TH]), "ctxT": b.din("ctxT", [D, 256]), "tabs": b.din("tabs", [128, 2, TH]),
        "w_in0": b.din("w_in0", [10, 128, 4096]), "w_pool": b.din("w_pool", [4, 128, 512]),
        "w_out0": b.din("w_out0", [8, 128, 4096]),
    }
    X = phase1(b, c, io)
    xout = b.dout("Xout", [128, 16, T])
    for t4 in range(4):
        b.store(xout[:, 4 * t4:4 * t4 + 4, :], X[:, 4 * t4:4 * t4 + 4, :], [("X", t) for t in range(4 * t4, 4 * t4 + 4)])
    print("arena peak KB", b.peak * 2 / 1024)
    return b.finish()


TC = T + 2
UCH = ((0, 342), (342, 684), (684, 1026))


def halo_select(b, c, edge_dram, name):
    s = b.s
    vecs = c["vecs"]
    E = b.sb(name + "E", [8, 16, 2], F32)
    b.load(E, edge_dram.rearrange("(r p) (t e) -> p r t e", p=128, e=2), [name + "E"], "ldE")
    tmpE = b.sb(name + "T", [8, 16], F32)
    xh = b.sb(name, [16, 2], F32)
    for side, (ohc, ecol) in enumerate(((V_OHL, 1), (V_OHR, 0))):
        s.op("dve", lambda e, ohc=ohc, ecol=ecol: e.tensor_tensor(
            out=tmpE, in0=E[:, :, :, ecol], in1=vecs[:, ohc:ohc + 8].unsqueeze(2).to_broadcast([128, 8, 16]), op=ALU.mult),
             reads=[name + "E", "vecs"], writes=[name + "T"])
        s.op("dve", lambda e, side=side: e.tensor_reduce(out=xh[:, :, side], in_=tmpE.rearrange("p r t -> p t r"),
                                                        axis=AX.X, op=ALU.add),
             reads=[name + "T"], writes=[name])
    b.free(name + "E", name + "T")
    return xh


def phase_ffn(b, c, io, X, l, xh):
    s = b.s
    vecs = c["vecs"]
    A2, S2 = mk_AS(b, c, l, 0, 3, 4, V_G2[l], "n2")
    gate2 = mk_gate(b, c, l, 5, "gate2")
    convp = b.sb("convp", [88, 4], F32)
    b.load(convp, io["convp"], ["convp"], "ldcv")
    sqr = Ring(b, "sq", 2, [512], BF16)
    tmpr = Ring(b, "tmp", 3, [512], F32)
    rr = Ring(b, "rr", 2, [512], F32)
    nrings = (sqr, tmpr, rr)
    h2 = b.sb("h2", [16, T], BF16)
    hh = b.sb("hh", [16, 2], BF16)
    hf = b.sb("hf", [16, 2], F32)
    norm_mod(b, c, [(xh, [xh_name(xh, b)], hh, ["hh"], 2)], A2, S2, "n2A", "n2S", nrings)
    for side, fc in enumerate((V_FL, V_FR)):
        s.op("dve", lambda e, side=side, fc=fc: e.tensor_scalar(
            out=hf[:, :, side], in0=hh[:, :, side], scalar1=vecs[:, fc:fc + 1], scalar2=None, op0=ALU.mult),
             reads=["hh", "vecs"], writes=["hf"])
    s.op("dve", lambda e: e.tensor_copy(out=hh, in_=hf), reads=["hf"], writes=["hh"])
    for ch in range(2):
        norm_mod(b, c, [(X[:, :, ch * 512:(ch + 1) * 512], [("X", t) for t in range(16)],
                         h2[:, :, ch * 512:(ch + 1) * 512], [("h2", ch)], 512)], A2, S2, "n2A", "n2S", nrings)
    sqr.free()
    tmpr.free()
    rr.free()
    UW = T + 8
    Ur = Ring(b, "U", 2, [UW], F32)
    accr = Ring(b, "acc", 4, [T], F32)
    sgr = Ring(b, "sg", 2, [T], F32)
    ar = Ring(b, "a", 8, [T], BF16)
    NG = NPAIR // 4

    def down_group(q, slots):
        for half in range(2):
            wt, wk = b.wload(io["w_down"][2 * q + half], 4096)
            wv = wt.rearrange("p (k m) -> p k m", m=1024)
            for mm in range(8):
                m = half * 8 + mm
                for ch in range(2):
                    pt, pk = b.psum()
                    for k in range(4):
                        av, ak = slots[k]
                        s.op("pe", lambda e, pt=pt, k=k, mm=mm, ch=ch, wv=wv, av=av: e.matmul(
                            pt[:, :], wv[:, k, mm * 128:(mm + 1) * 128], av[:, ch * 512:(ch + 1) * 512],
                            start=(k == 0), stop=(k == 3)), reads=[wk, ak], writes=[pk])
                    xs_ = X[:, m, ch * 512:(ch + 1) * 512]
                    s.op("dve", lambda e, pt=pt, m=m, xs_=xs_: e.scalar_tensor_tensor(
                        out=xs_, in0=pt[:, :], scalar=gate2[:, m:m + 1], in1=xs_, op0=ALU.mult, op1=ALU.add),
                         reads=[pk, "gate2", ("X", m)], writes=[("X", m)])

    prev = None
    for q in range(NG):
        slots = []
        for ii in range(4):
            i = 4 * q + ii
            wt, wk = b.wload(io["w_up"][i], 4096)
            wv = wt.rearrange("p (k m) -> p k m", m=256)
            accs = []
            for gv in range(2):
                mt = i + NPAIR * gv
                U, Uk = Ur.next()
                for ch in range(2):
                    pt, pk = b.psum()
                    for kt in range(16):
                        s.op("pe", lambda e, pt=pt, kt=kt, gv=gv, ch=ch, wv=wv: e.matmul(
                            pt[:, :], wv[:, kt, gv * 128:(gv + 1) * 128], h2[:, kt, ch * 512:(ch + 1) * 512],
                            start=(kt == 0), stop=(kt == 15)), reads=[wk, ("h2", ch)], writes=[pk])
                    s.op("act", lambda e, pt=pt, U=U, ch=ch: e.activation(out=U[:, 4 + ch * 512:4 + (ch + 1) * 512], in_=pt[:, :], func=AF.Copy),
                         reads=[pk], writes=[Uk])
                pt, pk = b.psum()
                for kt in range(16):
                    s.op("pe", lambda e, pt=pt, kt=kt, gv=gv, wv=wv: e.matmul(
                        pt[:, 0:2], wv[:, kt, gv * 128:(gv + 1) * 128], hh[:, kt, :], start=(kt == 0), stop=(kt == 15)),
                         reads=[wk, "hh"], writes=[pk])
                s.op("act", lambda e, pt=pt, U=U: e.activation(out=U[:, 3:4], in_=pt[:, 0:1], func=AF.Copy), reads=[pk], writes=[Uk])
                s.op("act", lambda e, pt=pt, U=U: e.activation(out=U[:, T + 4:T + 5], in_=pt[:, 1:2], func=AF.Copy), reads=[pk], writes=[Uk])
                acc, acck = accr.next()
                s.op("dve", lambda e, acc=acc, U=U, mt=mt: e.tensor_scalar(
                    out=acc, in0=U[:, 4:T + 4], scalar1=convp[:, mt, 1:2], scalar2=convp[:, mt, 3:4], op0=ALU.mult, op1=ALU.add),
                     reads=[Uk, "convp"], writes=[acck])
                s.op("dve", lambda e, acc=acc, U=U, mt=mt: e.scalar_tensor_tensor(
                    out=acc, in0=U[:, 3:T + 3], scalar=convp[:, mt, 0:1], in1=acc, op0=ALU.mult, op1=ALU.add),
                     reads=[Uk, "convp", acck], writes=[acck])
                s.op("dve", lambda e, acc=acc, U=U, mt=mt: e.scalar_tensor_tensor(
                    out=acc, in0=U[:, 5:T + 5], scalar=convp[:, mt, 2:3], in1=acc, op0=ALU.mult, op1=ALU.add),
                     reads=[Uk, "convp", acck], writes=[acck])
                accs.append((acc, acck))
            sg, sgk = sgr.next()
            s.op("act", lambda e, sg=sg, a0=accs[0][0]: e.activation(out=sg, in_=a0, func=AF.Silu), reads=[accs[0][1]], writes=[sgk])
            av, ak = ar.next()
            s.op("dve", lambda e, av=av, sg=sg, a1=accs[1][0]: e.tensor_tensor(out=av, in0=sg, in1=a1, op=ALU.mult),
                 reads=[sgk, accs[1][1]], writes=[ak])
            slots.append((av, ak))
        if prev is not None:
            down_group(*prev)
        prev = (q, slots)
    down_group(*prev)
    Ur.free()
    accr.free()
    sgr.free()
    ar.free()
    b.free("h2", "hh", "hf", "convp", "n2A", "n2S", "gate2")


def xh_name(xh, b):
    return b._xh_name


def host_w_ffn(inp, l):
    w_up = inp["f_w_up"][l]
    cols = np.arange(128)
    blocks = [(list(range(16)), np.concatenate([i * 128 + cols, DFF + i * 128 + cols])) for i in range(NPAIR)]
    wup = tile_w(w_up, blocks)
    w_down = inp["f_w_down"][l]
    c1024 = np.arange(1024)
    dblocks = []
    for q in range(NPAIR // 4):
        for half in range(2):
            dblocks.append(([4 * q + k for k in range(4)], half * 1024 + c1024))
    wdn = tile_w(w_down, dblocks)
    cw = inp["f_conv_w"][l]
    cb = inp["f_conv_b"][l]
    convp = np.stack([pvec(cw[0]), pvec(cw[1]), pvec(cw[2]), pvec(cb)], axis=-1)
    return {"w_up": wup, "w_down": wdn, "convp": np.ascontiguousarray(convp.astype(np.float32))}


def host_edges(xfull):
    E = np.empty((NCORES, 128, 16, 2), np.float32)
    for r in range(NCORES):
        first = xfull[T * r]
        last = xfull[T * r + T - 1]
        E[r, :, :, 0] = pvec(first)
        E[r, :, :, 1] = pvec(last)
    return np.ascontiguousarray(E.reshape(NCORES * 128, 32))


def load_X(b, src):
    X = b.sb("X", [16, T], F32)
    for t4 in range(4):
        b.load(X[:, 4 * t4:4 * t4 + 4, :], src[:, 4 * t4:4 * t4 + 4, :], [("X", t) for t in range(4 * t4, 4 * t4 + 4)], "ldX%d" % t4)
    return X


def store_X(b, X, dst):
    for t4 in range(4):
        b.store(dst[:, 4 * t4:4 * t4 + 4, :], X[:, 4 * t4:4 * t4 + 4, :], [("X", t) for t in range(4 * t4, 4 * t4 + 4)])


def build_ffn(l):
    b = Builder()
    c = load_common(b)
    io = {"w_up": b.din("w_up", [NPAIR, 128, 4096]), "w_down": b.din("w_down", [NPAIR // 2, 128, 4096]),
          "convp": b.din("convp", [128, 88, 4])}
    X = load_X(b, b.din("Xin", [128, 16, T]))
    b._xh_name = "xh"
    xh = halo_select(b, c, b.din("edges", [NCORES * 128, 32]), "xh")
    phase_ffn(b, c, io, X, l, xh)
    b.free("xh")
    store_X(b, X, b.dout("Xout", [128, 16, T]))
    print("arena peak KB", b.peak * 2 / 1024)
    return b.finish()


def fm_to_tokens(Xfm):
    return Xfm.transpose(2, 1, 0).reshape(Xfm.shape[2], D)


def tokens_to_fm(xt):
    return np.ascontiguousarray(xt.reshape(xt.shape[0], 16, 128).transpose(2, 1, 0))


def phase3_norm(b, c, X):
    A, S = mk_AS(b, c, 1, 0, 0, 1, V_G1[1], "m1")
    sqr = Ring(b, "sq", 2, [512], BF16)
    tmpr = Ring(b, "tmp", 3, [512], F32)
    rr = Ring(b, "rr", 2, [512], F32)
    h1 = b.sb("h1", [16, T], BF16)
    for ch in range(2):
        norm_mod(b, c, [(X[:, :, ch * 512:(ch + 1) * 512], [("X", t) for t in range(16)],
                         h1[:, :, ch * 512:(ch + 1) * 512], [("h1", ch)], 512)], A, S, "m1A", "m1S", (sqr, tmpr, rr))
    sqr.free()
    tmpr.free()
    rr.free()
    b.free("m1A", "m1S")
    return h1


def phase3_f(b, c, io, h1, fdst, fdt):
    s = b.s
    Fst = Ring(b, "Fst", 2, [T], fdt)
    for blk in range(4):
        wt, wk = b.wload(io["w_in1"][8 + blk], 4096)
        wv = wt.rearrange("p (k m) -> p k m", m=256)
        for mm in range(2):
            g = 2 * blk + mm
            fs, fk = Fst.next()
            for ch in range(2):
                pt, pk = b.psum()
                for kt in range(16):
                    s.op("pe", lambda e, pt=pt, kt=kt, mm=mm, ch=ch, wv=wv: e.matmul(
                        pt[:, :], wv[:, kt, mm * 128:(mm + 1) * 128], h1[:, kt, ch * 512:(ch + 1) * 512],
                        start=(kt == 0), stop=(kt == 15)), reads=[wk, ("h1", ch)], writes=[pk])
                s.op("act", lambda e, pt=pt, fs=fs, ch=ch: e.activation(out=fs[:, ch * 512:(ch + 1) * 512], in_=pt[:, :], func=AF.Copy),
                     reads=[pk], writes=[fk])
            b.store(fdst[g], fs, [fk], key="outf" + fk)
    return Fst


def phase3_uv(b, c, io, h1):
    s = b.s
    gvB = b.sb("gvB", [1024], F32)
    b.load(gvB, io["gvB"], ["gvB"], "ldg")
    VG = b.sb("VG", [8, 1024], F32)
    Vtm = b.sb("Vtm", [8, 1024], BF16)
    ss = b.sb("ss", [8], F32)
    junk = b.sb("junk", [1024], BF16)
    for blk in range(4):
        wt, wk = b.wload(io["w_in1"][4 + blk], 4096)
        wv = wt.rearrange("p (k m) -> p k m", m=256)
        for k in range(8):
            pt, pk = b.psum()
            for kt in range(16):
                s.op("pe", lambda e, pt=pt, kt=kt, k=k, wv=wv: e.matmul(
                    pt[:, 0:256], h1[:, kt, k * 128:(k + 1) * 128], wv[:, kt, :], start=(kt == 0), stop=(kt == 15)),
                     reads=[wk, ("h1", k // 4)], writes=[pk])
            s.op("act", lambda e, pt=pt, k=k, blk=blk: e.activation(out=VG[:, k, blk * 256:(blk + 1) * 256], in_=pt[:, 0:256], func=AF.Gelu),
                 reads=[pk], writes=[("VG", k)])
    for k in range(8):
        s.op("act", lambda e, k=k: e.activation(out=junk, in_=VG[:, k, :], func=AF.Square, accum_out=ss[:, k:k + 1]),
             reads=[("VG", k)], writes=["junk", ("ss", k)])
    epsc = c["epsc"]
    s.op("act", lambda e: e.activation(out=ss, in_=ss, func=AF.Sqrt, bias=epsc[:, 2:3], scale=1.0),
         reads=[("ss", k) for k in range(8)] + ["epsc"], writes=["ssr"])
    s.op("dve", lambda e: e.reciprocal(out=ss, in_=ss), reads=["ssr"], writes=["ssr"])
    s.op("dve", lambda e: e.tensor_scalar(out=ss, in0=ss, scalar1=32.0, scalar2=None, op0=ALU.mult), reads=["ssr"], writes=["ssr"])
    for k in range(8):
        s.op("dve", lambda e, k=k: e.scalar_tensor_tensor(out=Vtm[:, k, :], in0=VG[:, k, :], scalar=ss[:, k:k + 1], in1=gvB,
                                                          op0=ALU.mult, op1=ALU.mult),
             reads=[("VG", k), "ssr", "gvB"], writes=[("Vtm", k)])
    b.free("VG", "junk")
    U1 = b.sb("U1", [8, T], BF16)
    for blk in range(4):
        wt, wk = b.wload(io["w_in1"][blk], 4096)
        wv = wt.rearrange("p (k m) -> p k m", m=256)
        for mm in range(2):
            ct = 2 * blk + mm
            for ch in range(2):
                pt, pk = b.psum()
                for kt in range(16):
                    s.op("pe", lambda e, pt=pt, kt=kt, mm=mm, ch=ch, wv=wv: e.matmul(
                        pt[:, :], wv[:, kt, mm * 128:(mm + 1) * 128], h1[:, kt, ch * 512:(ch + 1) * 512],
                        start=(kt == 0), stop=(kt == 15)), reads=[wk, ("h1", ch)], writes=[pk])
                s.op("act", lambda e, pt=pt, ct=ct, ch=ch: e.activation(out=U1[:, ct, ch * 512:(ch + 1) * 512], in_=pt[:, :], func=AF.Gelu),
                     reads=[pk], writes=[("U1", ct)])
    b.free("h1")
    wsT = b.sb("wsT", [4, 128], BF16)
    s.dma("pool", lambda e: e.dma_start(out=wsT, in_=io["wsT"]), key="ldws", writes=["wsT"])
    bsp = b.sb("bsp", [4, 128], F32)
    b.load(bsp, io["bspB"], ["bsp"], "ldbs")
    CAT = b.sb("CAT", [16, T], BF16)
    tmpr = Ring(b, "tmp", 2, [512], F32)
    for ct in range(8):
        g = ct // 2
        for half in range(2):
            pt, pk = b.psum()
            for kk in range(4):
                k = half * 4 + kk
                s.op("pe", lambda e, pt=pt, kk=kk, k=k, ct=ct, g=g: e.matmul(
                    pt[:, kk * 128:(kk + 1) * 128], Vtm[:, k, ct * 128:(ct + 1) * 128], wsT[:, g, :], start=True, stop=True),
                     reads=[("Vtm", k), "wsT"], writes=[pk])
            tmp, tk = tmpr.next()
            s.op("dve", lambda e, pt=pt, tmp=tmp, g=g: e.tensor_tensor(
                out=tmp.rearrange("p (a b) -> p a b", b=128), in0=pt[:, :].rearrange("p (a b) -> p a b", b=128),
                in1=bsp[:, g, :].unsqueeze(1).to_broadcast([128, 4, 128]), op=ALU.add), reads=[pk, "bsp"], writes=[tk])
            s.op("dve", lambda e, tmp=tmp, ct=ct, half=half: e.tensor_tensor(
                out=CAT[:, ct, half * 512:(half + 1) * 512], in0=tmp, in1=U1[:, ct, half * 512:(half + 1) * 512], op=ALU.mult),
                 reads=[tk, ("U1", ct)], writes=[("CAT", ct)])
    tmpr.free()
    b.free("U1", "Vtm", "gvB", "ss", "wsT", "bsp")
    return CAT


def phase5(b, c, io, X, CAT, zsrc, z_cast):
    s = b.s
    ZT = b.sb("ZT", [8, T], BF16)
    if z_cast:
        s.dma("pool", lambda e: e.dma_start(out=ZT, in_=zsrc), key="ldz", writes=["ZT"])
    else:
        b.load(ZT, zsrc, ["ZT"], "ldz")
    gate = mk_gate(b, c, 1, 2, "gate1b")
    ei = [0]
    for blk in range(2):
        wt, wk = b.wload(io["w_four"][blk], 4096)
        wv = wt.rearrange("p (k m) -> p k m", m=512)
        for mm in range(4):
            m1 = 4 * blk + mm
            for ch in range(2):
                pt, pk = b.psum()
                for g in range(8):
                    s.op("pe", lambda e, pt=pt, g=g, mm=mm, ch=ch, wv=wv: e.matmul(
                        pt[:, :], wv[:, g, mm * 128:(mm + 1) * 128], ZT[:, g, ch * 512:(ch + 1) * 512],
                        start=(g == 0), stop=(g == 7)), reads=[wk, "ZT"], writes=[pk])
                ei[0] += 1
                dst = CAT[:, 8 + m1, ch * 512:(ch + 1) * 512]
                if ei[0] % 2:
                    s.op("act", lambda e, pt=pt, dst=dst: e.activation(out=dst, in_=pt[:, :], func=AF.Copy), reads=[pk], writes=[("CAT", 8 + m1)])
                else:
                    s.op("dve", lambda e, pt=pt, dst=dst: e.tensor_copy(out=dst, in_=pt[:, :]), reads=[pk], writes=[("CAT", 8 + m1)])
    for blk in range(8):
        wt, wk = b.wload(io["w_out1"][blk], 4096)
        wv = wt.rearrange("p (k m) -> p k m", m=256)
        for mm in range(2):
            m = 2 * blk + mm
            for half in range(2):
                pt, pk = b.psum()
                for kt in range(16):
                    s.op("pe", lambda e, pt=pt, kt=kt, mm=mm, half=half, wv=wv: e.matmul(
                        pt[:, :], wv[:, kt, mm * 128:(mm + 1) * 128], CAT[:, kt, half * 512:(half + 1) * 512],
                        start=(kt == 0), stop=(kt == 15)), reads=[wk, ("CAT", kt)], writes=[pk])
                xs_ = X[:, m, half * 512:(half + 1) * 512]
                s.op("dve", lambda e, pt=pt, m=m, xs_=xs_: e.scalar_tensor_tensor(
                    out=xs_, in0=pt[:, :], scalar=gate[:, m:m + 1], in1=xs_, op0=ALU.mult, op1=ALU.add),
                     reads=[pk, "gate1b", ("X", m)], writes=[("X", m)])
    b.free("ZT", "CAT", "gate1b")


def phase_dft(b, c, io, fsrc, f_cast, zdst, zdt):
    s = b.s
    fsb = b.sb("fsb", [SEQ], BF16)
    if f_cast:
        s.dma("pool", lambda e: e.dma_start(out=fsb, in_=fsrc), key="ldf", writes=["fsb"])
    else:
        b.load(fsb, fsrc, ["fsb"], "ldf")
    dftc = b.sb("dftc", [512], BF16)
    s.dma("pool", lambda e: e.dma_start(out=dftc, in_=io["dftc"]), key="lddc", writes=["dftc"])
    fcs = dftc[:, 0:256].rearrange("p (r m) -> p r m", r=2)
    CS64 = dftc[0:64, 256:384]
    SC64 = dftc[0:64, 384:512]
    fv = fsb.rearrange("p (h l) -> p l h", l=128)
    Asb = b.sb("Asb", [128, 128], BF16)
    ei = [0]

    def evac(dst, src, reads, writes, scale=None):
        ei[0] += 1
        if scale is not None:
            s.op("act", lambda e: e.activation(out=dst, in_=src, func=AF.Copy, scale=float(scale)), reads=reads, writes=writes)
        elif ei[0] % 2:
            s.op("act", lambda e: e.activation(out=dst, in_=src, func=AF.Copy), reads=reads, writes=writes)
        else:
            s.op("dve", lambda e: e.tensor_copy(out=dst, in_=src), reads=reads, writes=writes)

    for qd in range(4):
        Gq = b.sb("Gq", [128, 64], BF16)
        rhsC = fcs[:, :, 32 * qd:32 * qd + 32]
        for n8 in range(16):
            pt, pk = b.psum()
            for j in range(8):
                nl = 8 * n8 + j
                s.op("pe", lambda e, pt=pt, j=j, nl=nl, rhsC=rhsC: e.matmul(pt[0:64, j * 64:(j + 1) * 64], fv[:, nl, :], rhsC, start=True, stop=True),
                     reads=["fsb", "dftc"], writes=[pk])
            evac(Gq[0:64, 8 * n8:8 * n8 + 8, :], pt[0:64, :].rearrange("p (a b) -> p a b", b=64), [pk], [("Gq", n8)])
        for m4 in range(8):
            pt, pk = b.psum()
            for j in range(4):
                mi = 4 * m4 + j
                s.op("pe", lambda e, pt=pt, j=j, mi=mi, Gq=Gq: e.matmul(pt[:, j * 128:(j + 1) * 128], Gq[0:64, :, mi], CS64, start=True, stop=False),
                     reads=[("Gq", n8) for n8 in range(16)] + ["dftc"], writes=[pk])
                s.op("pe", lambda e, pt=pt, j=j, mi=mi, Gq=Gq: e.matmul(pt[:, j * 128:(j + 1) * 128], Gq[0:64, :, 32 + mi], SC64, start=False, stop=True),
                     reads=[("Gq", n8) for n8 in range(16)] + ["dftc"], writes=[pk])
            m0 = 32 * qd + 4 * m4
            evac(Asb[:, m0:m0 + 4, :], pt[:, :].rearrange("p (a b) -> p a b", b=128), [pk], [("Asb", m0 // 4)])
        b.free("Gq")
    b.free("fsb")
    Zsb = b.sb("Zsb", [SEQ], zdt)
    Zv = Zsb.rearrange("p (a b) -> p a b", b=64)
    akeys = [("Asb", i) for i in range(32)]
    for kg in range(2):
        wc, wck = b.wload(io["cosn"][0, kg], 4096)
        wn, wnk = b.wload(io["cosn"][1, kg], 4096)
        cv = wc.rearrange("p (b a) -> p b a", a=128)
        nv = wn.rearrange("p (b a) -> p b a", a=128)
        for k4 in range(8):
            pt, pk = b.psum()
            for j in range(4):
                kl = 4 * k4 + j
                kb = 32 * kg + kl
                s.op("pe", lambda e, pt=pt, j=j, kl=kl, kb=kb, cv=cv: e.matmul(pt[:, j * 128:(j + 1) * 128], Asb[:, :, kb], cv[:, kl, :], start=True, stop=False),
                     reads=akeys + [wck], writes=[pk])
                s.op("pe", lambda e, pt=pt, j=j, kl=kl, kb=kb, nv=nv: e.matmul(pt[:, j * 128:(j + 1) * 128], Asb[:, :, 64 + kb], nv[:, kl, :], start=False, stop=True),
                     reads=akeys + [wnk], writes=[pk])
            kb0 = 32 * kg + 4 * k4
            evac(Zv[:, :, kb0:kb0 + 4], pt[:, :].rearrange("p (b a) -> p a b", a=128), [pk], [("Zsb", kb0 // 4)], scale=2.0 ** -10)
    zkeys = [("Zsb", i) for i in range(16)]
    for q4 in range(4):
        b.store(zdst[:, q4 * 2048:(q4 + 1) * 2048], Zsb[:, q4 * 2048:(q4 + 1) * 2048], zkeys, key="outz")
    b.free("Asb", "Zsb", "dftc")


def host_dft_consts():
    c = np.arange(128)
    m = np.arange(128)
    th = 2 * np.pi * np.outer(c, m) / 128.0
    dftc = np.zeros((128, 512), np.float32)
    dftc[:, 0:128] = np.cos(th)
    dftc[:, 128:256] = np.sin(th)
    nh = np.arange(64)
    kb = np.arange(64)
    t64 = 2 * np.pi * np.outer(nh, kb) / 64.0
    dftc[0:64, 256:320] = np.cos(t64)
    dftc[0:64, 320:384] = np.sin(t64)
    dftc[0:64, 384:448] = -np.sin(t64)
    dftc[0:64, 448:512] = np.cos(t64)
    nl = np.arange(128)[:, None, None]
    kbb = np.arange(64)[None, :, None]
    ka = np.arange(128)[None, None, :]
    k = 64 * ka + kbb
    phi = 2 * np.pi * ((nl * k) % SEQ) / float(SEQ)
    cosn = np.stack([np.cos(phi), -np.sin(phi)], axis=0).astype(np.float32)
    cosn = cosn.reshape(2, 128, 2, 32 * 128).transpose(0, 2, 1, 3)
    return dftc, np.ascontiguousarray(cosn)


def host_w_l1mix(inp):
    cols = np.arange(256)
    c512 = np.arange(512)
    ws = inp["c_w_spatial"][0]
    wsT = np.ascontiguousarray(ws.transpose(2, 0, 1)).astype(np.float32)
    bsp = np.ascontiguousarray(np.broadcast_to(inp["c_b_spatial"][0][None], (128, 4, 128))).astype(np.float32)
    gvB = np.ascontiguousarray(np.broadcast_to(inp["c_v_norm_g"][0][None], (128, 1024))).astype(np.float32)
    return {
        "w_in1": tile_w(inp["cd_w_in"][0], [(list(range(16)), 256 * bb + cols) for bb in range(12)]),
        "w_four": tile_w(inp["d_w_fourier"][0], [(list(range(8)), 512 * bb + c512) for bb in range(2)]),
        "w_out1": tile_w(inp["cd_w_out"][0], [(list(range(16)), 256 * bb + cols) for bb in range(8)]),
        "wsT": wsT, "bspB": bsp, "gvB": gvB,
    }


def host_cb1():
    cb = np.zeros((128, 256), np.float32)
    cb[:, 0:128] = 1.0
    cb[:, 128:256] = np.eye(128, dtype=np.float32)
    return cb


def build_p3f():
    b = Builder()
    c = load_common(b, need_cbf=False)
    io = {"w_in1": b.din("w_in1", [12, 128, 4096])}
    X = load_X(b, b.din("Xin", [128, 16, T]))
    h1 = phase3_norm(b, c, X)
    fout = b.dout("Fout", [8, 128, T])
    phase3_f(b, c, io, h1, [fout[g] for g in range(8)], F32)
    print("arena peak KB", b.peak * 2 / 1024)
    return b.finish()


def build_dft():
    b = Builder()
    c = {}
    io = {"dftc": b.din("dftc", [128, 512]), "cosn": b.din("cosn", [2, 2, 128, 4096])}
    phase_dft(b, c, io, b.din("Fin", [128, SEQ]), True, b.dout("Zout", [128, SEQ]), F32)
    print("arena peak KB", b.peak * 2 / 1024)
    return b.finish()


def build_p5():
    b = Builder()
    c = load_common(b, need_cbf=False)
    io = {"w_in1": b.din("w_in1", [12, 128, 4096]), "w_four": b.din("w_four", [2, 128, 4096]),
          "w_out1": b.din("w_out1", [8, 128, 4096]), "wsT": b.din("wsT", [128, 4, 128]),
          "bspB": b.din("bspB", [128, 4, 128]), "gvB": b.din("gvB", [128, 1024])}
    X = load_X(b, b.din("Xin", [128, 16, T]))
    h1 = phase3_norm(b, c, X)
    CAT = phase3_uv(b, c, io, h1)
    phase5(b, c, io, X, CAT, b.din("Zin", [128, 8, T]), True)
    store_X(b, X, b.dout("Xout", [128, 16, T]))
    print("arena peak KB", b.peak * 2 / 1024)
    return b.finish()


_CACHE = {}


def _prog(name, fn):
    if name not in _CACHE:
        _CACHE[name] = fn()
    return _CACHE[name]


def _run(nc, maps):
    return run_bass_kernel_spmd(nc, maps, core_ids=list(range(NCORES))).results


def kernel_unfused(**inp):
    inp = {k: np.asarray(v) for k, v in inp.items()}
    R = range(NCORES)
    res = _run(_prog("mod", lambda: (lambda b: (build_mod(b), b.finish())[1])(Builder())), host_mod_inputs(inp))
    modv = gather_mod(res)
    vecs = [host_vecs(inp, r) for r in R]
    cb1 = host_cb1()
    W0 = host_w_l0mix(inp)
    ctxT = np.ascontiguousarray(inp["ctx"][0].T)
    maps = []
    for r in R:
        m = {"modv": modv, "vecs": vecs[r], "cb1": cb1, "cbf": host_cbf(r), "tabs": host_tabs(r),
             "xT": host_xT(inp, r), "ctxT": ctxT}
        m.update(W0)
        maps.append(m)
    res = _run(_prog("p1", build_p1), maps)
    Xfm = [res[r]["Xout"] for r in R]

    def ffn(l, Xfm):
        Wf = host_w_ffn(inp, l)
        E = np.empty((NCORES, 128, 16, 2), np.float32)
        for r in R:
            E[r, :, :, 0] = Xfm[r][:, :, 0]
            E[r, :, :, 1] = Xfm[r][:, :, T - 1]
        edges = np.ascontiguousarray(E.reshape(NCORES * 128, 32))
        maps = []
        for r in R:
            m = {"modv": modv, "vecs": vecs[r], "cb1": cb1, "cbf": host_cbf(r), "edges": edges, "Xin": Xfm[r]}
            m.update(Wf)
            maps.append(m)
        res = _run(_prog("ffn%d" % l, lambda: build_ffn(l)), maps)
        return [res[r]["Xout"] for r in R]

    Xfm = ffn(0, Xfm)
    W1 = host_w_l1mix(inp)
    maps = [{"modv": modv, "vecs": vecs[r], "cb1": cb1, "Xin": Xfm[r], "w_in1": W1["w_in1"]} for r in R]
    res = _run(_prog("p3f", build_p3f), maps)
    Fo = [res[r]["Fout"] for r in R]
    dftc, cosn = host_dft_consts()
    maps = [{"dftc": dftc, "cosn": cosn, "Fin": np.ascontiguousarray(np.concatenate([Fo[r][g] for r in R], axis=1))} for g in R]
    res = _run(_prog("dft", build_dft), maps)
    Zo = [res[g]["Zout"] for g in R]
    maps = []
    for r in R:
        Zin = np.ascontiguousarray(np.stack([Zo[g][:, T * r:T * (r + 1)] for g in R], axis=1))
        m = {"modv": modv, "vecs": vecs[r], "cb1": cb1, "Xin": Xfm[r], "Zin": Zin}
        m.update(W1)
        maps.append(m)
    res = _run(_prog("p5", build_p5), maps)
    Xfm = [res[r]["Xout"] for r in R]
    Xfm = ffn(1, Xfm)
    out = np.concatenate([fm_to_tokens(Xfm[r]) for r in R], axis=0)
    return np.ascontiguousarray(out[None].astype(np.float32))


def kernel(**inputs):
    return kernel_unfused(**inputs)
```
